# Optimizing a Trainium2 kernel written in Bass

```python
import math
import jax
import jax.numpy as jnp
from jax import lax
import numpy as np


D_MODEL = 4096
BATCH = 4
SEQ = 2048
DEPTH = 2
DEC_BATCH = 8
DEC_SEQ = 4
PAST_LEN = 16384
PAGE_SIZE = 128

HEAD_DIM = 128
MIX_WIDTH = D_MODEL
CONV_CH = MIX_WIDTH // 4
ATTN_HEADS = (3 * MIX_WIDTH // 8) // HEAD_DIM
MLSTM_HEADS = (MIX_WIDTH - CONV_CH - ATTN_HEADS * HEAD_DIM) // HEAD_DIM
ATTN_W = ATTN_HEADS * HEAD_DIM
MLSTM_W = MLSTM_HEADS * HEAD_DIM
CONV_WIDTH = 31
DILATED_PATTERNS = ((128, 1), (512, 4), (2048, 16))
MAX_WINDOW = 2048
ATTN_Q_BLOCK = 128
MLSTM_CHUNK = 64
PEER_HEADS = 8
PEER_N_KEYS = 128
PEER_N_EXPERTS = PEER_N_KEYS * PEER_N_KEYS
PEER_QUERY_DIM = 256
PEER_HALF = PEER_QUERY_DIM // 2
PEER_TOPK = 16
PEER_TOKEN_BLOCK = 128
DEEPNORM_ALPHA = (2 * DEPTH) ** 0.25
DEEPNORM_BETA = (8 * DEPTH) ** -0.25
LN_EPS = 1e-5

OFF_GLU = 0
OFF_ATT = OFF_GLU + 2 * CONV_CH
OFF_MLS = OFF_ATT + 3 * ATTN_W
OFF_GATE = OFF_MLS + 4 * MLSTM_W
P_IN = OFF_GATE + 2 * MLSTM_HEADS

kernel_name = "hymba_conv_dilattn_mlstm_peer_step"


def layer_norm(x, g, b):
    xf = x.astype(jnp.float32)
    mu = jnp.mean(xf, -1, keepdims=True)
    var = jnp.mean(jnp.square(xf - mu), -1, keepdims=True)
    return ((xf - mu) * lax.rsqrt(var + LN_EPS) * g + b).astype(x.dtype)


def head_norm(h, g):
    hf = h.astype(jnp.float32)
    mu = jnp.mean(hf, -1, keepdims=True)
    var = jnp.mean(jnp.square(hf - mu), -1, keepdims=True)
    return ((hf - mu) * lax.rsqrt(var + LN_EPS) * g).astype(h.dtype)


def conformer_conv(u, prev, w, b, g, beta):
    ext = jnp.concatenate([prev.astype(u.dtype), u], axis=1)
    y = lax.conv_general_dilated(ext, w[:, None, :].astype(u.dtype), window_strides=(1,), padding='VALID',
                                 dimension_numbers=('NWC', 'WIO', 'NWC'),
                                 feature_group_count=u.shape[-1]) + b
    y = jax.nn.silu(layer_norm(y, g, beta))
    return y, ext[:, -(CONV_WIDTH - 1):]


def dilated_attention(q, k_ext, v_ext, n_prev):
    B, T, H, hd = q.shape
    qb = ATTN_Q_BLOCK if T % ATTN_Q_BLOCK == 0 else T
    nb = T // qb
    q_blocks = q.reshape(B, nb, qb, H, hd).transpose(1, 0, 2, 3, 4)
    q_idx = (n_prev + jnp.arange(T, dtype=jnp.int32)).reshape(nb, qb)
    scale = hd ** -0.5

    def one_block(args):
        qblk, qi = args
        outs, lses = [], []
        for window, dil in DILATED_PATTERNS:
            offs = dil * jnp.arange(window // dil + 1, dtype=jnp.int32)
            kidx = qi[:, None] - offs[None, :]
            valid = kidx >= 0
            kidx = jnp.maximum(kidx, 0)
            kg = k_ext[:, kidx]
            vg = v_ext[:, kidx]
            s = jnp.einsum('bqhd,bqjhd->bhqj', qblk, kg).astype(jnp.float32) * scale
            s = jnp.where(valid[None, None], s, -jnp.inf)
            mx = jnp.max(s, -1, keepdims=True)
            e = jnp.exp(s - mx)
            den = jnp.sum(e, -1, keepdims=True)
            p = (e / den).astype(vg.dtype)
            outs.append(jnp.einsum('bhqj,bqjhd->bqhd', p, vg).astype(jnp.float32))
            lses.append((mx + jnp.log(den))[..., 0])
        wts = jax.nn.softmax(jnp.stack(lses), axis=0).transpose(0, 1, 3, 2)[..., None]
        return jnp.sum(wts * jnp.stack(outs), axis=0).astype(q.dtype)

    out = lax.map(one_block, (q_blocks, q_idx))
    return out.transpose(1, 0, 2, 3, 4).reshape(B, T, H, hd)


def mlstm_chunkwise(q, k, v, i_pre, logf, c0, n0, m0):
    B, T, H, hd = q.shape
    L = MLSTM_CHUNK if T % MLSTM_CHUNK == 0 else T
    nc = T // L
    f32 = jnp.float32

    def to_chunks(a):
        a = a.reshape((B, nc, L) + a.shape[2:])
        return jnp.moveaxis(jnp.moveaxis(a, 1, 0), 3, 2)

    qc = to_chunks(q.astype(f32))
    kc = to_chunks(k.astype(f32) * hd ** -0.5)
    vc = to_chunks(v.astype(f32))
    ic = to_chunks(i_pre.astype(f32))
    fc = to_chunks(logf.astype(f32))
    causal = jnp.tril(jnp.ones((L, L), dtype=bool))

    def step(carry, inp):
        c, n, m = carry
        qt, kt, vt, it, ft = inp
        b = jnp.cumsum(ft, axis=-1)
        a_inter = b + m[..., None]
        a_intra = jnp.where(causal, b[..., :, None] - b[..., None, :] + it[..., None, :], -jnp.inf)
        m_t = jnp.maximum(a_inter, jnp.max(a_intra, -1))
        w_intra = jnp.exp(a_intra - m_t[..., None])
        w_inter = jnp.exp(a_inter - m_t)
        qk = jnp.einsum('bhtd,bhsd->bhts', qt, kt) * w_intra
        num = w_inter[..., None] * jnp.einsum('bhtd,bhde->bhte', qt, c) + jnp.einsum('bhts,bhse->bhte', qk, vt)
        qn = w_inter * jnp.einsum('bhtd,bhd->bht', qt, n) + jnp.sum(qk, -1)
        h = num / jnp.maximum(jnp.abs(qn), jnp.exp(-m_t))[..., None]
        m_new = m_t[..., -1]
        decay = jnp.exp(b[..., -1] + m - m_new)
        g = jnp.exp(b[..., -1:] - b + it - m_new[..., None])
        c_new = decay[..., None, None] * c + jnp.einsum('bhs,bhsd,bhse->bhde', g, kt, vt)
        n_new = decay[..., None] * n + jnp.einsum('bhs,bhsd->bhd', g, kt)
        return (c_new, n_new, m_new), h

    (c, n, m), h = lax.scan(step, (c0.astype(f32), n0.astype(f32), m0.astype(f32)), (qc, kc, vc, ic, fc))
    h = jnp.moveaxis(jnp.moveaxis(h, 2, 3), 0, 1).reshape(B, T, H, hd)
    return h.astype(q.dtype), c.astype(c0.dtype), n.astype(n0.dtype), m.astype(m0.dtype)


def peer(x, wq, subkeys, u_tab, v_tab):
    B, T, D = x.shape
    xt = x.reshape(B * T, D)
    n_tok = B * T
    nb = -(-n_tok // PEER_TOKEN_BLOCK)
    xt = jnp.pad(xt, ((0, nb * PEER_TOKEN_BLOCK - n_tok), (0, 0))).reshape(nb, PEER_TOKEN_BLOCK, D)

    def one_block(xb):
        q = (xb @ wq).reshape(PEER_TOKEN_BLOCK, PEER_HEADS, 2, PEER_HALF)
        s = jnp.einsum('nhpd,hpkd->nhpk', q, subkeys).astype(jnp.float32)
        sv, si = lax.top_k(s, PEER_TOPK)
        cand = sv[:, :, 0, :, None] + sv[:, :, 1, None, :]
        cid = si[:, :, 0, :, None] * PEER_N_KEYS + si[:, :, 1, None, :]
        cv, ci = lax.top_k(cand.reshape(PEER_TOKEN_BLOCK, PEER_HEADS, -1), PEER_TOPK)
        eid = jnp.take_along_axis(cid.reshape(PEER_TOKEN_BLOCK, PEER_HEADS, -1), ci, axis=-1)
        gate = jax.nn.softmax(cv, axis=-1)
        ug = u_tab[eid]
        vg = v_tab[eid]
        act = jax.nn.gelu(jnp.einsum('nd,nhkd->nhk', xb, ug).astype(jnp.float32), approximate=False)
        return jnp.einsum('nhk,nhkd->nd', (gate * act).astype(vg.dtype), vg)

    out = lax.map(one_block, xt).reshape(nb * PEER_TOKEN_BLOCK, D)[:n_tok]
    return out.reshape(B, T, D).astype(x.dtype)


def trunk_layer(x, k_prev, v_prev, conv_prev, c0, n0, m0,
                w_in, conv_w, conv_b, conv_ln_g, conv_ln_b, b_i, b_f, norm_g, w_out,
                ln1_g, ln1_b, peer_wq, peer_subkeys, peer_u, peer_v, ln2_g, ln2_b):
    B, T, _ = x.shape
    proj = jnp.einsum('btd,dp->btp', x, w_in)
    glu = proj[..., OFF_GLU:OFF_GLU + CONV_CH] * jax.nn.sigmoid(proj[..., OFF_GLU + CONV_CH:OFF_ATT])
    ya, conv_new = conformer_conv(glu, conv_prev, conv_w, conv_b, conv_ln_g, conv_ln_b)
    qkv = proj[..., OFF_ATT:OFF_MLS].reshape(B, T, 3, ATTN_HEADS, HEAD_DIM)
    q, k, v = qkv[:, :, 0], qkv[:, :, 1], qkv[:, :, 2]
    k_ext = jnp.concatenate([k_prev.astype(k.dtype), k], axis=1)
    v_ext = jnp.concatenate([v_prev.astype(v.dtype), v], axis=1)
    yb = dilated_attention(q, k_ext, v_ext, k_prev.shape[1]).reshape(B, T, ATTN_W)
    mq = proj[..., OFF_MLS:OFF_GATE].reshape(B, T, 4, MLSTM_HEADS, HEAD_DIM)
    gates = proj[..., OFF_GATE:].astype(jnp.float32).reshape(B, T, 2, MLSTM_HEADS)
    i_pre = gates[:, :, 0] + b_i
    logf = jax.nn.log_sigmoid(gates[:, :, 1] + b_f)
    h, c_new, n_new, m_new = mlstm_chunkwise(mq[:, :, 0], mq[:, :, 1], mq[:, :, 2], i_pre, logf, c0, n0, m0)
    yc = head_norm(jax.nn.sigmoid(mq[:, :, 3]) * h, norm_g).reshape(B, T, MLSTM_W)
    mix = jnp.einsum('btm,md->btd', jnp.concatenate([ya, yb, yc], axis=-1), w_out)
    x = layer_norm(DEEPNORM_ALPHA * x + mix, ln1_g, ln1_b)
    x = layer_norm(DEEPNORM_ALPHA * x + peer(x, peer_wq, peer_subkeys, peer_u, peer_v), ln2_g, ln2_b)
    return x, k, v, conv_new, c_new, n_new, m_new


def setup_inputs(seed: int = 0) -> dict:
    key = jax.random.key(seed)
    ks = jax.random.split(key, 32)
    f32 = jnp.float32

    def nrm(k, shape, s):
        return jax.random.normal(k, shape, f32) * s

    win_rows = min(MAX_WINDOW, PAST_LEN)
    return {
        'x_prompt': nrm(ks[0], (BATCH, SEQ, D_MODEL), 1.0),
        'x_sample': nrm(ks[1], (DEC_BATCH, DEC_SEQ, D_MODEL), 1.0),
        'cache_attn_k': nrm(ks[2], (DEPTH, DEC_BATCH, win_rows, ATTN_HEADS, HEAD_DIM), 1.0),
        'cache_attn_v': nrm(ks[3], (DEPTH, DEC_BATCH, win_rows, ATTN_HEADS, HEAD_DIM), 1.0),
        'state_conv': nrm(ks[4], (DEPTH, DEC_BATCH, CONV_WIDTH - 1, CONV_CH), 0.5),
        'state_mlstm_c': nrm(ks[5], (DEPTH, DEC_BATCH, MLSTM_HEADS, HEAD_DIM, HEAD_DIM), 0.1),
        'state_mlstm_n': nrm(ks[6], (DEPTH, DEC_BATCH, MLSTM_HEADS, HEAD_DIM), 0.1),
        'state_mlstm_m': nrm(ks[7], (DEPTH, DEC_BATCH, MLSTM_HEADS), 0.5),
        'w_in': nrm(ks[8], (DEPTH, D_MODEL, P_IN), D_MODEL ** -0.5),
        'conv_w': nrm(ks[9], (DEPTH, CONV_WIDTH, CONV_CH), CONV_WIDTH ** -0.5),
        'conv_b': nrm(ks[10], (DEPTH, CONV_CH), 0.01),
        'conv_ln_g': 1.0 + nrm(ks[11], (DEPTH, CONV_CH), 0.02),
        'conv_ln_b': nrm(ks[12], (DEPTH, CONV_CH), 0.01),
        'mlstm_b_i': nrm(ks[13], (DEPTH, MLSTM_HEADS), 0.1),
        'mlstm_b_f': 3.0 + nrm(ks[14], (DEPTH, MLSTM_HEADS), 0.5),
        'mlstm_norm_g': 1.0 + nrm(ks[15], (DEPTH, MLSTM_HEADS, HEAD_DIM), 0.02),
        'w_out': nrm(ks[16], (DEPTH, MIX_WIDTH, D_MODEL), DEEPNORM_BETA * MIX_WIDTH ** -0.5),
        'ln1_g': 1.0 + nrm(ks[17], (DEPTH, D_MODEL), 0.02),
        'ln1_b': nrm(ks[18], (DEPTH, D_MODEL), 0.01),
        'peer_wq': nrm(ks[19], (DEPTH, D_MODEL, PEER_HEADS * PEER_QUERY_DIM), D_MODEL ** -0.5),
        'peer_subkeys': nrm(ks[20], (DEPTH, PEER_HEADS, 2, PEER_N_KEYS, PEER_HALF), PEER_HALF ** -0.5),
        'peer_u': nrm(ks[21], (DEPTH, PEER_N_EXPERTS, D_MODEL), D_MODEL ** -0.5),
        'peer_v': nrm(ks[22], (DEPTH, PEER_N_EXPERTS, D_MODEL), DEEPNORM_BETA * (PEER_HEADS * PEER_TOPK) ** -0.5),
        'ln2_g': 1.0 + nrm(ks[23], (DEPTH, D_MODEL), 0.02),
        'ln2_b': nrm(ks[24], (DEPTH, D_MODEL), 0.01),
    }


def reference(x_prompt, x_sample, cache_attn_k, cache_attn_v, state_conv, state_mlstm_c, state_mlstm_n,
              state_mlstm_m, w_in, conv_w, conv_b, conv_ln_g, conv_ln_b, mlstm_b_i, mlstm_b_f, mlstm_norm_g,
              w_out, ln1_g, ln1_b, peer_wq, peer_subkeys, peer_u, peer_v, ln2_g, ln2_b):
    B, T, _ = x_prompt.shape
    dt = x_prompt.dtype
    empty_kv = jnp.zeros((B, 0, ATTN_HEADS, HEAD_DIM), dt)
    conv_zero = jnp.zeros((B, CONV_WIDTH - 1, CONV_CH), dt)
    c_zero = jnp.zeros((B, MLSTM_HEADS, HEAD_DIM, HEAD_DIM), jnp.float32)
    n_zero = jnp.zeros((B, MLSTM_HEADS, HEAD_DIM), jnp.float32)
    m_zero = jnp.zeros((B, MLSTM_HEADS), jnp.float32)
    win_p = min(MAX_WINDOW, T)

    yp, ys = x_prompt, x_sample
    kp_l, vp_l, ks_l, vs_l, cvp_l, cvs_l = [], [], [], [], [], []
    cp_l, cs_l, np_l, ns_l, mp_l, ms_l = [], [], [], [], [], []
    for l in range(DEPTH):
        wl = (w_in[l], conv_w[l], conv_b[l], conv_ln_g[l], conv_ln_b[l], mlstm_b_i[l], mlstm_b_f[l],
              mlstm_norm_g[l], w_out[l], ln1_g[l], ln1_b[l], peer_wq[l], peer_subkeys[l], peer_u[l],
              peer_v[l], ln2_g[l], ln2_b[l])
        yp, kp, vp, cvp, cp, np_, mp = trunk_layer(yp, empty_kv, empty_kv, conv_zero, c_zero, n_zero, m_zero, *wl)
        ys, ks_, vs_, cvs, cs, ns, ms = trunk_layer(ys, cache_attn_k[l], cache_attn_v[l], state_conv[l],
                                                    state_mlstm_c[l], state_mlstm_n[l], state_mlstm_m[l], *wl)
        kp_l.append(kp[:, T - win_p:]); vp_l.append(vp[:, T - win_p:])
        ks_l.append(ks_); vs_l.append(vs_)
        cvp_l.append(cvp); cvs_l.append(cvs)
        cp_l.append(cp); cs_l.append(cs)
        np_l.append(np_); ns_l.append(ns)
        mp_l.append(mp); ms_l.append(ms)

    k_prompt = jnp.stack(kp_l)
    v_prompt = jnp.stack(vp_l)
    k_sample = jnp.stack(ks_l)
    v_sample = jnp.stack(vs_l)
    conv_prompt = jnp.stack(cvp_l)
    conv_sample = jnp.stack(cvs_l)
    c_prompt = jnp.stack(cp_l)
    c_sample = jnp.stack(cs_l)
    n_prompt = jnp.stack(np_l)
    n_sample = jnp.stack(ns_l)
    m_prompt = jnp.stack(mp_l)
    m_sample = jnp.stack(ms_l)
    return (yp, ys, k_prompt, v_prompt, k_sample, v_sample, conv_prompt, conv_sample,
            c_prompt, c_sample, n_prompt, n_sample, m_prompt, m_sample)
```

```python
import math
from contextlib import ExitStack
from types import SimpleNamespace

import os
import numpy as np
import concourse.bass as bass
import concourse.mybir as mybir
from concourse.bass_utils import run_bass_kernel_spmd

F32 = mybir.dt.float32
BF16 = mybir.dt.bfloat16
AF = mybir.ActivationFunctionType
ALU = mybir.AluOpType
AX = mybir.AxisListType

LN_EPS = 1e-5
NEG = -1.0e30


class Buf:
    __slots__ = ("name", "w", "r")

    def __init__(self, name=""):
        self.name = name
        self.w = None
        self.r = {}


class Sched:
    ENG = ("pe", "act", "dve", "pool", "sp")

    def __init__(self, nc, es):
        self.nc = nc
        self.es = es
        self.q = {e: [] for e in self.ENG}
        self.sems = {}
        self.cnt = {}
        self.known = {e: {} for e in self.ENG}
        self.last_dma = {}
        self.enabled = True
        for e in self.ENG:
            self._newsem("E_" + e)

    def _newsem(self, key):
        self.sems[key] = self.es.enter_context(self.nc.semaphore(key))
        self.cnt[key] = 0

    def _deps(self, reads, writes):
        toks = []
        for b in reads:
            if b.w is not None:
                toks.append(b.w)
        for b in writes:
            if b.w is not None:
                toks.append(b.w)
            toks.extend(b.r.items())
        return toks

    def _waits(self, eng, toks):
        need = {}
        kn = self.known[eng]
        for k, v in toks:
            if kn.get(k, 0) < v and need.get(k, 0) < v:
                need[k] = v
        for k, v in need.items():
            kn[k] = v
        return [(self.sems[k], v) for k, v in need.items()]

    def _mark(self, tok, reads, writes):
        k, v = tok
        for b in reads:
            if b.r.get(k, 0) < v:
                b.r[k] = v
        for b in writes:
            b.w = tok
            b.r = {}

    def op(self, eng, fn, reads=(), writes=()):
        if not self.enabled:
            return None
        waits = self._waits(eng, self._deps(reads, writes))
        key = "E_" + eng
        self.cnt[key] += 1
        tok = (key, self.cnt[key])
        sem = self.sems[key]

        def run(e):
            for s, v in waits:
                e.wait_ge(s, v)
            fn(e).then_inc(sem, 1)

        self.q[eng].append(run)
        self._mark(tok, reads, writes)
        return tok

    def dma(self, eng, semkey, fn, reads=(), writes=(), n=1):
        if not self.enabled:
            return None
        if semkey not in self.sems:
            self._newsem(semkey)
        toks = self._deps(reads, writes)
        if semkey in self.last_dma:
            toks.append(self.last_dma[semkey])
        waits = self._waits(eng, toks)
        self.cnt[semkey] += 16 * n
        tok = (semkey, self.cnt[semkey])
        self.last_dma[semkey] = tok
        sem = self.sems[semkey]

        def run(e):
            for s, v in waits:
                e.wait_ge(s, v)
            for ins in fn(e):
                ins.then_inc(sem, 16)

        self.q[eng].append(run)
        self._mark(tok, reads, writes)
        return tok

    def barrier(self):
        if not self.enabled:
            return
        allk = [(k, v) for k, v in self.cnt.items() if v > 0]
        for e in self.ENG:
            waits = self._waits(e, allk)
            if waits:
                def run(eng, waits=waits):
                    for s, v in waits:
                        eng.wait_ge(s, v)
                self.q[e].append(run)

    def emit(self):
        self.barrier()
        with self.nc.Block() as block:
            @block.tensor
            def _(e):
                for f in self.q["pe"]:
                    f(e)

            @block.scalar
            def _(e):
                for f in self.q["act"]:
                    f(e)

            @block.vector
            def _(e):
                for f in self.q["dve"]:
                    f(e)

            @block.gpsimd
            def _(e):
                for f in self.q["pool"]:
                    f(e)

            @block.sync
            def _(e):
                for f in self.q["sp"]:
                    f(e)


def I_MM(out, lhsT, rhs, start=True, stop=True):
    return lambda e: e.matmul(out, lhsT=lhsT, rhs=rhs, start=start, stop=stop)


def I_MMS(items):
    def f(e):
        ins = None
        for (out, lhsT, rhs, st, sp) in items:
            ins = e.matmul(out, lhsT=lhsT, rhs=rhs, start=st, stop=sp)
        return ins
    return f


def I_TR(out, in_, ident):
    return lambda e: e.transpose(out=out, in_=in_, identity=ident)


def I_ACT(out, in_, func, **kw):
    return lambda e: e.activation(out=out, in_=in_, func=func, **kw)


def I_AMUL(out, in_, mul):
    return lambda e: e.mul(out=out, in_=in_, mul=mul)


def I_TT(out, a, b, op):
    return lambda e: e.tensor_tensor(out=out, in0=a, in1=b, op=op)


def I_TS(out, a, s1, s2, op0, op1=None):
    if op1 is None:
        return lambda e: e.tensor_scalar(out=out, in0=a, scalar1=s1, scalar2=None, op0=op0)
    return lambda e: e.tensor_scalar(out=out, in0=a, scalar1=s1, scalar2=s2, op0=op0, op1=op1)


def I_STT(out, a, s, b, op0, op1):
    return lambda e: e.scalar_tensor_tensor(out=out, in0=a, scalar=s, in1=b, op0=op0, op1=op1)


def I_CP(out, in_):
    return lambda e: e.tensor_copy(out=out, in_=in_)


def I_MS(ap, val):
    return lambda e: e.memset(ap, val)


def I_DMA(pairs, **kw):
    return lambda e: [e.dma_start(out=o, in_=i, **kw) for (o, i) in pairs]


def I_MAX(out, in_):
    return lambda e: e.max(out=out, in_=in_)


def I_MR(out, rep, vals, imm):
    return lambda e: e.match_replace(out=out, in_to_replace=rep, in_values=vals, imm_value=imm)


def I_RED(out, in_, op):
    return lambda e: e.tensor_reduce(out=out, in_=in_, axis=AX.X, op=op)


def I_RCP(out, in_):
    return lambda e: e.reciprocal(out=out, in_=in_)


def I_SCAN(out, d0, d1, init, op0, op1):
    return lambda e: e.tensor_tensor_scan(out=out, data0=d0, data1=d1, initial=init, op0=op0, op1=op1)


def mkcfg(D=4096, S=2048, W=2048, L=2, TS=4, B=4, DB=8):
    c = SimpleNamespace()
    c.D, c.S, c.W, c.L, c.TS, c.B, c.DB = D, S, W, L, TS, B, DB
    c.KD = D // 128
    c.CC = D // 4
    c.NCC = c.CC // 128
    c.AH = (3 * D // 8) // 128
    c.AW = c.AH * 128
    c.MH = (D - c.CC - c.AW) // 128
    c.MW = c.MH * 128
    c.NA = S + TS
    c.NB = S // 128
    c.NC = S // 64
    c.NCHI = 2 * c.NCC + 3 * c.AH + 4 * c.MH
    c.P_IN = c.NCHI * 128 + 2 * c.MH
    c.groups = [(g * 512, min(512, S - g * 512)) for g in range((S + 511) // 512)] + [(S, TS)]
    c.blocks = [(b * 128, 128) for b in range(c.NB)] + [(S, TS)]
    c.E = 16384
    nt = (c.NA + 511) // 512
    base, rem = divmod(c.NA, nt)
    c.tiles = []
    t0 = 0
    for i in range(nt):
        w = base + (1 if i < rem else 0)
        c.tiles.append((t0, w))
        t0 += w
    c.XW = 384 + max(S, W) + 512 + 64
    KD, NCC = c.KD, c.NCC
    c.o_ln1g, c.o_ln1b, c.o_ln2g, c.o_ln2b = 0, KD, 2 * KD, 3 * KD
    c.o_cb = 4 * KD
    c.o_cg = c.o_cb + NCC
    c.o_cbeta = c.o_cg + NCC
    c.o_cw = c.o_cbeta + NCC
    c.o_ng = c.o_cw + 31 * NCC
    c.NV = c.o_ng + c.MH
    c.alpha = float((2 * L) ** 0.25)
    c.stages = ("conv", "attn", "mlstm", "p2")
    c.nlayers_run = L
    return c


def build(c):
    nc = bass.Bass("TRN2", target_bir_lowering=False)
    D, S_, W, L, TS, KD, NA = c.D, c.S, c.W, c.L, c.TS, c.KD, c.NA
    NCC, AH, MH, CC, AW, NB, NC = c.NCC, c.AH, c.MH, c.CC, c.AW, c.NB, c.NC
    XW = c.XW
    HS = 128.0 ** -0.5

    def din(name, shape, dt=F32):
        return nc.dram_tensor(name, list(shape), dt, kind="ExternalInput").ap()

    def dout(name, shape, dt=F32):
        return nc.dram_tensor(name, list(shape), dt, kind="ExternalOutput").ap()

    xT_in = din("xT", [D, NA])
    w_in = din("w_in", [L, c.NCHI, 128, KD, 128])
    w_gate = din("w_gate", [L, 2, 128, KD, MH])
    tiny = "p2" not in c.stages
    w_out = din("w_out", [L, KD, 128, KD, 128] if not tiny else [L, 1, 1, 1, 1])
    wq = din("wq", [L, 16, 128, KD, 128] if not tiny else [L, 1, 1, 1, 1])
    uT = din("uT", [L, 128, 128, KD, 128] if not tiny else [L, 1, 1, 1, 1])
    vtab = din("vtab", [L, 128, 128, D] if not tiny else [L, 1, 1, 1])
    subk = din("subk", [L, 128, 16, 128])
    vecs = din("vecs", [L, 128, c.NV])
    gbias = din("gbias", [L, MH, 2])
    ckT = din("ckT", [L, AH, 128, W])
    cvv = din("cvv", [L, W, AW])
    cconvT = din("cconvT", [L, CC, 30])
    cst_c = din("cst_c", [L, MH, 128, 129])
    cst_m = din("cst_m", [L, MH, 1])
    consts = din("consts", [128, 256])
    cmask = din("cmask", [128, XW])

    yT = dout("yT", [D, NA])
    kT_o = dout("kT_o", [L, AW, NA])
    vT_o = dout("vT_o", [L, AW, NA])
    convT_o = dout("convT_o", [L, 2, CC, 30])
    c_o = dout("c_o", [L, 2, MH, 128, 129])
    m_o = dout("m_o", [L, MH, 2])

    mixT = nc.dram_tensor("mixT", [D, NA], BF16, kind="Internal").ap()
    xsc = [nc.dram_tensor("xsc%d" % i, [D, NA], F32, kind="Internal").ap() for i in range(2)]

    with ExitStack() as es:
        S = Sched(nc, es)

        uid = [0]

        def T(stack, name, shape, dt=F32):
            uid[0] += 1
            return stack.enter_context(nc.sbuf_tensor("%s_%d" % (name, uid[0]), list(shape), dt))

        cst_f = T(es, "cst_f", [128, 256])
        cst_b = T(es, "cst_b", [128, 256], BF16)
        ones_b = T(es, "ones_b", [128, 128], BF16)
        ones_f = T(es, "ones_f", [128, 128])
        cm = T(es, "cm", [128, XW], BF16)
        vec = [T(es, "vec%d" % l, [128, c.NV]) for l in range(L)]
        b_cst, b_ones, b_cm, b_vec = Buf(), Buf(), Buf(), Buf()
        ident_f, mask2_f = cst_f[:, 0:128], cst_f[:, 128:256]
        ident_b, mask2_b = cst_b[:, 0:128], cst_b[:, 128:256]
        S.dma("sp", "const", I_DMA([(cst_f[:], consts)] + [(vec[l][:], vecs[l]) for l in range(L)]),
              writes=[b_cst, b_vec], n=1 + L)
        S.dma("pool", "constb", I_DMA([(cst_b[:], consts), (cm[:], cmask)], max_dma_last_dim=4096),
              writes=[b_cst, b_cm], n=2)
        S.op("dve", I_MS(ones_b[:], 1.0), writes=[b_ones])
        S.op("dve", I_MS(ones_f[:], 1.0), writes=[b_ones])

        ps = [es.enter_context(nc.psum_tensor("ps%d" % i, [128, 512], F32)) for i in range(8)]
        psB = [Buf("ps%d" % i) for i in range(8)]

        b_mix = [Buf() for _ in range(KD)]
        b_xs = {}

        def xbufs(ap_id):
            return [b_xs.setdefault((ap_id, i), Buf()) for i in range(len(c.tiles))]

        state = SimpleNamespace(wi=0, pi=0)

        def phase1(l, x_src, x_src_id):
            V = vec[l]
            with ExitStack() as sc:
                xb_all = T(sc, "xb_all", [128, KD, NA], BF16)
                b_xb = Buf()
                xsrc_v = x_src.rearrange("(k p) n -> p k n", p=128)
                kstep = max(1, KD // 4)
                for k0 in range(0, KD, kstep):
                    S.dma("pool", "xb", I_DMA([(xb_all[:, k, :], xsrc_v[:, k, :]) for k in range(k0, k0 + kstep)], max_dma_last_dim=4096),
                          reads=xbufs(x_src_id), writes=[b_xb], n=kstep)
                NWB = 2
                wbuf = [T(sc, "wbuf%d" % i, [128, KD, 128], BF16) for i in range(NWB)]
                wbB = [Buf() for _ in range(NWB)]

                def load_w(src, M=128):
                    i = state.wi % NWB
                    state.wi += 1
                    S.dma("pool", "w%d" % i, I_DMA([(wbuf[i][:, :, 0:M], src)], max_dma_last_dim=4096), writes=[wbB[i]])
                    return wbuf[i], wbB[i]

                def proj(src, M, consume, npsum=3):
                    wt, wtb = load_w(src, M)
                    for gi, (c0, n) in enumerate(c.groups):
                        pi = state.pi % npsum
                        state.pi += 1
                        items = [(ps[pi][0:M, 0:n], wt[:, k, 0:M], xb_all[:, k, c0:c0 + n], k == 0, k == KD - 1)
                                 for k in range(KD)]
                        S.op("pe", I_MMS(items), reads=[wtb, b_xb], writes=[psB[pi]])
                        consume(gi, c0, n, ps[pi][0:M, 0:n], psB[pi])

                with ExitStack() as s2:
                    S.enabled = "conv" in c.stages
                    XS = 30 + S_
                    ext = T(s2, "ext", [128, XS + 30 + TS])
                    tmpA = T(s2, "tmpA", [128, NA])
                    sg = T(s2, "sg", [128, 512])
                    S12 = T(s2, "S12", [128, 2, NA])
                    yst = T(s2, "yst", [128, NA], BF16)
                    ysq = T(s2, "ysq", [128, NA], BF16)
                    b_ext, b_tmpA, b_sg, b_S12, b_yst, b_ysq = (Buf() for _ in range(6))
                    S.op("dve", I_MS(ext[:, 0:30], 0.0), writes=[b_ext])

                    def ecol(c0):
                        return 30 + c0 if c0 < S_ else XS + 30 + (c0 - S_)

                    for j in range(NCC):
                        S.dma("sp", "cst", I_DMA([(ext[:, XS:XS + 30], cconvT[l, j * 128:(j + 1) * 128, :])]), writes=[b_ext])

                        def cons_a(gi, c0, n, pap, pb):
                            S.op("act", I_ACT(tmpA[:, c0:c0 + n], pap, AF.Copy), reads=[pb], writes=[b_tmpA])
                        proj(w_in[l, j], 128, cons_a)

                        def cons_b(gi, c0, n, pap, pb):
                            S.op("act", I_ACT(sg[:, 0:n], pap, AF.Sigmoid), reads=[pb], writes=[b_sg])
                            S.op("dve", I_TT(ext[:, ecol(c0):ecol(c0) + n], tmpA[:, c0:c0 + n], sg[:, 0:n], ALU.mult),
                                 reads=[b_tmpA, b_sg], writes=[b_ext])
                        proj(w_in[l, NCC + j], 128, cons_b)
                        wcol = c.o_cw + j * 31
                        for (o0, i0, n) in ((0, 0, S_), (S_, XS, TS)):
                            S.op("dve", I_TS(tmpA[:, o0:o0 + n], ext[:, i0:i0 + n], V[:, wcol:wcol + 1],
                                             V[:, c.o_cb + j:c.o_cb + j + 1], ALU.mult, ALU.add),
                                 reads=[b_ext, b_vec], writes=[b_tmpA])
                            for tap in range(1, 31):
                                S.op("dve", I_STT(tmpA[:, o0:o0 + n], ext[:, i0 + tap:i0 + tap + n],
                                                  V[:, wcol + tap:wcol + tap + 1], tmpA[:, o0:o0 + n], ALU.mult, ALU.add),
                                     reads=[b_ext, b_vec, b_tmpA], writes=[b_tmpA])
                        S.dma("sp", "o_cv", I_DMA([
                            (convT_o[l, 0, j * 128:(j + 1) * 128, :], ext[:, S_:S_ + 30]),
                            (convT_o[l, 1, j * 128:(j + 1) * 128, :], ext[:, XS + TS:XS + TS + 30])]),
                            reads=[b_ext], n=2)
                        S.op("act", I_ACT(yst[:], tmpA[:], AF.Copy), reads=[b_tmpA], writes=[b_yst])
                        S.op("act", I_ACT(ysq[:], tmpA[:], AF.Square), reads=[b_tmpA], writes=[b_ysq])
                        for (c0, n) in c.groups:
                            S.op("pe", I_MM(ps[3][:, 0:n], ones_b[:], yst[:, c0:c0 + n]), reads=[b_ones, b_yst], writes=[psB[3]])
                            S.op("pe", I_MM(ps[4][:, 0:n], ones_b[:], ysq[:, c0:c0 + n]), reads=[b_ones, b_ysq], writes=[psB[4]])
                            for q_, pq in ((0, 3), (1, 4)):
                                if j == 0:
                                    S.op("dve", I_CP(S12[:, q_, c0:c0 + n], ps[pq][:, 0:n]), reads=[psB[pq]], writes=[b_S12])
                                else:
                                    S.op("dve", I_TT(S12[:, q_, c0:c0 + n], S12[:, q_, c0:c0 + n], ps[pq][:, 0:n], ALU.add),
                                         reads=[psB[pq], b_S12], writes=[b_S12])
                        S.dma("sp", "mixw", I_DMA([(mixT[j * 128:(j + 1) * 128, :], yst[:])]), reads=[b_yst], writes=[b_mix[j]])
                    S.op("dve", I_TS(S12[:, 0, :], S12[:, 0, :], 1.0 / CC, None, ALU.mult), reads=[b_S12], writes=[b_S12])
                    S.op("dve", I_TT(tmpA[:], S12[:, 0, :], S12[:, 0, :], ALU.mult), reads=[b_S12], writes=[b_tmpA])
                    S.op("dve", I_STT(S12[:, 1, :], S12[:, 1, :], 1.0 / CC, tmpA[:], ALU.mult, ALU.subtract),
                         reads=[b_S12, b_tmpA], writes=[b_S12])
                    S.op("dve", I_TS(S12[:, 1, :], S12[:, 1, :], LN_EPS, None, ALU.add), reads=[b_S12], writes=[b_S12])
                    S.op("act", I_ACT(S12[:, 1, :], S12[:, 1, :], AF.Sqrt), reads=[b_S12], writes=[b_S12])
                    S.op("dve", I_RCP(S12[:, 1, :], S12[:, 1, :]), reads=[b_S12], writes=[b_S12])
                    for j in range(NCC):
                        S.dma("sp", "mixr", I_DMA([(yst[:], mixT[j * 128:(j + 1) * 128, :])]), reads=[b_mix[j]], writes=[b_yst])
                        S.op("dve", I_TT(tmpA[:], yst[:], S12[:, 0, :], ALU.subtract), reads=[b_yst, b_S12], writes=[b_tmpA])
                        S.op("dve", I_TT(tmpA[:], tmpA[:], S12[:, 1, :], ALU.mult), reads=[b_tmpA, b_S12], writes=[b_tmpA])
                        S.op("act", I_ACT(ysq[:], tmpA[:], AF.Silu, scale=V[:, c.o_cg + j:c.o_cg + j + 1],
                                          bias=V[:, c.o_cbeta + j:c.o_cbeta + j + 1]),
                             reads=[b_tmpA, b_vec], writes=[b_ysq])
                        S.dma("sp", "mixw", I_DMA([(mixT[j * 128:(j + 1) * 128, :], ysq[:])]), reads=[b_ysq], writes=[b_mix[j]])
                    S.barrier()

                with ExitStack() as s2:
                    S.enabled = "attn" in c.stages
                    qT = T(s2, "qT", [128, NA], BF16)
                    kT = T(s2, "kT", [128, NA], BF16)
                    vTb = T(s2, "vTb", [128, NA], BF16)
                    stg = [T(s2, "stg%d" % i, [128, 512]) for i in range(2)]
                    Vtm = T(s2, "Vtm", [128, NB + 1, 128], BF16)
                    ckt = T(s2, "ckt", [128, W], BF16)
                    cvt = T(s2, "cvt", [128, W // 128, 128], BF16)
                    eT = [T(s2, "eT%d" % i, [128, 512], BF16) for i in range(2)]
                    rden = T(s2, "rden", [128, 512])
                    ybT = T(s2, "ybT", [128, NA], BF16)
                    b_qT, b_kT, b_vTb, b_Vtm, b_ckt, b_cvt, b_rden, b_ybT = (Buf() for _ in range(8))
                    b_stg = [Buf(), Buf()]
                    b_eT = [Buf(), Buf()]
                    cnt = SimpleNamespace(stg=0, e=0)
                    OFF_Q, OFF_K, OFF_V = 2 * NCC, 2 * NCC + AH, 2 * NCC + 2 * AH
                    for h in range(AH):
                        SKIP = os.environ.get("ATT_SKIP", "")
                        if "c" not in SKIP:
                            S.dma("pool", "ck", I_DMA([(ckt[:], ckT[l, h])], max_dma_last_dim=4096), writes=[b_ckt])
                        if "d" not in SKIP:
                            S.dma("pool", "cv", I_DMA([(cvt[:], cvv[l, :, h * 128:(h + 1) * 128].rearrange("(c p) d -> p c d", p=128))]),
                                  writes=[b_cvt])

                        def cons_q(gi, c0, n, pap, pb):
                            S.op("act", I_AMUL(qT[:, c0:c0 + n], pap, HS), reads=[pb], writes=[b_qT])
                        if "q" not in SKIP:
                            proj(w_in[l, OFF_Q + h], 128, cons_q)

                        def mk_cons_kv(dstb, b_dstb, out_d, hh_=h):
                            def cons(gi, c0, n, pap, pb):
                                si = cnt.stg % 2
                                cnt.stg += 1
                                if "A" not in os.environ.get("ATT_SKIP", ""):
                                    S.op("act", I_ACT(stg[si][:, 0:n], pap, AF.Copy), reads=[pb], writes=[b_stg[si]])
                                if "D" not in os.environ.get("ATT_SKIP", ""):
                                    S.op("dve", I_CP(dstb[:, c0:c0 + n], stg[si][:, 0:n]), reads=[b_stg[si]], writes=[b_dstb])
                                if "o" not in os.environ.get("ATT_SKIP", ""):
                                    S.dma("sp", "kvo", I_DMA([(out_d[l, hh_ * 128:(hh_ + 1) * 128, c0:c0 + n], stg[si][:, 0:n])]),
                                          reads=[b_stg[si]])
                            return cons
                        if "k" not in SKIP:
                            proj(w_in[l, OFF_K + h], 128, mk_cons_kv(kT, b_kT, kT_o))
                            proj(w_in[l, OFF_V + h], 128, mk_cons_kv(vTb, b_vTb, vT_o))
                        for bi, (c0, n) in enumerate(c.blocks if "v" not in SKIP else []):
                            pv = ps[3][:].bitcast(BF16)
                            S.op("pe", I_TR(pv[0:n, 0:128], vTb[:, c0:c0 + n], ident_b), reads=[b_vTb, b_cst], writes=[psB[3]])
                            S.op("act", I_ACT(Vtm[0:n, bi, :], pv[0:n, 0:128], AF.Copy), reads=[psB[3]], writes=[b_Vtm])

                        def attend(q0, nq, keysets):
                            nks = len(keysets)
                            for ki, (kap, vap, nk, x0, rb) in enumerate(keysets):
                                pi = 5 + (cnt.e % 2)
                                ei = cnt.e % 2
                                cnt.e += 1
                                S.op("pe", I_MM(ps[pi][0:nk, 0:nq], kap, qT[:, q0:q0 + nq]), reads=[b_qT] + rb, writes=[psB[pi]])
                                S.op("act", I_ACT(eT[ei][0:nk, 0:nq], ps[pi][0:nk, 0:nq], AF.Exp), reads=[psB[pi]], writes=[b_eT[ei]])
                                S.op("dve", I_TT(eT[ei][0:nk, 0:nq], eT[ei][0:nk, 0:nq], cm[0:nk, x0:x0 + nq], ALU.mult),
                                     reads=[b_eT[ei], b_cm], writes=[b_eT[ei]])
                                S.op("pe", I_MMS([(ps[3][:, 0:nq], vap, eT[ei][0:nk, 0:nq], ki == 0, ki == nks - 1)]),
                                     reads=[b_eT[ei]] + rb, writes=[psB[3]])
                                S.op("pe", I_MMS([(ps[4][:, 0:nq], ones_b[0:nk, :], eT[ei][0:nk, 0:nq], ki == 0, ki == nks - 1)]),
                                     reads=[b_eT[ei], b_ones], writes=[psB[4]])
                            S.op("dve", I_RCP(rden[:, 0:nq], ps[4][:, 0:nq]), reads=[psB[4]], writes=[b_rden])
                            S.op("dve", I_TT(ybT[:, q0:q0 + nq], ps[3][:, 0:nq], rden[:, 0:nq], ALU.mult),
                                 reads=[psB[3], b_rden], writes=[b_ybT])

                        SKIP = os.environ.get("ATT_SKIP", "")
                        for (q0, nq) in (c.groups[:-1] if "p" not in SKIP else []):
                            ks = []
                            for kc in range((q0 + nq + 127) // 128):
                                k0 = kc * 128
                                nk = min(128, S_ - k0)
                                ks.append((kT[:, k0:k0 + nk], Vtm[0:nk, kc, :], nk, q0 - k0 + 384, [b_kT, b_Vtm]))
                            attend(q0, nq, ks)
                        ks = []
                        for kc in range(W // 128):
                            ks.append((ckt[:, kc * 128:(kc + 1) * 128], cvt[:, kc, :], 128, W - kc * 128 + 384, [b_ckt, b_cvt]))
                        if "n" not in SKIP:
                            ks.append((kT[:, S_:S_ + TS], Vtm[0:TS, NB, :], TS, 384, [b_kT, b_Vtm]))
                        if "s" not in SKIP:
                            attend(S_, TS, ks)
                        r0 = CC + h * 128
                        S.dma("sp", "mixw", I_DMA([(mixT[r0:r0 + 128, :], ybT[:])]), reads=[b_ybT], writes=[b_mix[r0 // 128]])
                    S.barrier()

                with ExitStack() as s2:
                    S.enabled = "mlstm" in c.stages
                    GRE = T(s2, "GRE", [128, NB + 1, 3, MH])
                    decb = T(s2, "decb", [128, MH, NC + 1])
                    b_GRE, b_decb = Buf(), Buf()
                    OFF_M = 2 * NCC + 3 * AH
                    with ExitStack() as s3:
                        gA = T(s3, "gA", [MH, NA])
                        gB = T(s3, "gB", [MH, NA])
                        gC = T(s3, "gC", [MH, NA])
                        gM = T(s3, "gM", [MH, NA])
                        gsm = T(s3, "gsm", [MH, 8])
                        mend = T(s3, "mend", [MH, NC + 1])
                        mprev = T(s3, "mprev", [MH, NC + 1])
                        Dg = T(s3, "Dg", [MH, MH, NC + 1])
                        mout = T(s3, "mout", [MH, 2])
                        b_gA, b_gB, b_gC, b_gM, b_gsm, b_mend, b_mprev, b_Dg, b_mout = (Buf() for _ in range(9))
                        S.dma("sp", "cst", I_DMA([(gsm[:, 0:2], gbias[l]), (gsm[:, 3:4], cst_m[l])]), writes=[b_gsm], n=2)
                        S.op("dve", I_TS(gsm[:, 2:3], gsm[:, 1:2], -1.0, None, ALU.mult), reads=[b_gsm], writes=[b_gsm])
                        S.op("dve", I_MS(gsm[:, 4:5], 1.0), reads=[b_gsm], writes=[b_gsm])
                        S.op("dve", I_MS(gsm[:, 5:6], 0.0), reads=[b_gsm], writes=[b_gsm])

                        def cons_i(gi, c0, n, pap, pb):
                            S.op("act", I_ACT(gA[:, c0:c0 + n], pap, AF.Identity, bias=gsm[:, 0:1]), reads=[pb, b_gsm], writes=[b_gA])
                        proj(w_gate[l, 0], MH, cons_i)

                        def cons_f(gi, c0, n, pap, pb):
                            S.op("act", I_ACT(gB[:, c0:c0 + n], pap, AF.Exp, bias=gsm[:, 2:3], scale=-1.0), reads=[pb, b_gsm], writes=[b_gB])
                        proj(w_gate[l, 1], MH, cons_f)
                        S.op("act", I_ACT(gB[:], gB[:], AF.Ln, bias=1.0), reads=[b_gB], writes=[b_gB])
                        for (c0, n, m0) in ((0, S_, gsm[:, 5:6]), (S_, TS, gsm[:, 3:4])):
                            S.op("dve", I_SCAN(gC[:, c0:c0 + n], gsm[:, 4:5].to_broadcast([MH, n]), gB[:, c0:c0 + n],
                                               0.0, ALU.mult, ALU.add), reads=[b_gB, b_gsm], writes=[b_gC])
                            S.op("dve", I_TT(gA[:, c0:c0 + n], gA[:, c0:c0 + n], gC[:, c0:c0 + n], ALU.add),
                                 reads=[b_gA, b_gC], writes=[b_gA])
                            S.op("dve", I_SCAN(gM[:, c0:c0 + n], gA[:, c0:c0 + n], gA[:, c0:c0 + n], m0, ALU.max, ALU.max),
                                 reads=[b_gA, b_gsm], writes=[b_gM])
                        S.op("dve", I_TT(gC[:], gM[:], gC[:], ALU.subtract), reads=[b_gM, b_gC], writes=[b_gC])
                        S.op("dve", I_CP(mout[:, 0:1], gC[:, S_ - 1:S_]), reads=[b_gC], writes=[b_mout])
                        S.op("dve", I_CP(mout[:, 1:2], gC[:, NA - 1:NA]), reads=[b_gC], writes=[b_mout])
                        S.dma("sp", "o_m", I_DMA([(m_o[l], mout[:])]), reads=[b_mout])
                        S.op("dve", I_CP(mend[:, 0:NC], gM[:, 0:S_].rearrange("p (c t) -> p c t", t=64)[:, :, 63]),
                             reads=[b_gM], writes=[b_mend])
                        S.op("dve", I_CP(mend[:, NC:NC + 1], gM[:, NA - 1:NA]), reads=[b_gM], writes=[b_mend])
                        S.op("dve", I_MS(mprev[:, 0:1], 0.0), writes=[b_mprev])
                        if NC > 1:
                            S.op("dve", I_CP(mprev[:, 1:NC], mend[:, 0:NC - 1]), reads=[b_mend], writes=[b_mprev])
                        S.op("dve", I_CP(mprev[:, NC:NC + 1], gsm[:, 3:4]), reads=[b_gsm], writes=[b_mprev])
                        S.op("dve", I_TT(mprev[:], mprev[:], mend[:], ALU.subtract), reads=[b_mprev, b_mend], writes=[b_mprev])
                        S.op("act", I_ACT(mprev[:], mprev[:], AF.Exp), reads=[b_mprev], writes=[b_mprev])
                        S.op("act", I_ACT(gC[:], gC[:], AF.Exp, scale=-1.0), reads=[b_gC], writes=[b_gC])
                        for (c0, n, cs, ncn, tl) in ((0, S_, 0, NC, 64), (S_, TS, NC, 1, TS)):
                            me_b = mend[:, cs:cs + ncn].unsqueeze(2).to_broadcast([MH, ncn, tl])
                            uA = gA[:, c0:c0 + n].rearrange("p (c t) -> p c t", t=tl)
                            uM = gM[:, c0:c0 + n].rearrange("p (c t) -> p c t", t=tl)
                            uB = gB[:, c0:c0 + n].rearrange("p (c t) -> p c t", t=tl)
                            S.op("dve", I_TT(uA, uA, me_b, ALU.subtract), reads=[b_gA, b_mend], writes=[b_gA])
                            S.op("dve", I_TT(uB, me_b, uM, ALU.subtract), reads=[b_gM, b_mend], writes=[b_gB])
                        S.op("act", I_ACT(gA[:], gA[:], AF.Exp), reads=[b_gA], writes=[b_gA])
                        S.op("act", I_ACT(gB[:], gB[:], AF.Exp), reads=[b_gB], writes=[b_gB])
                        for bi, (c0, n) in enumerate(c.blocks):
                            for qi, (src, sb) in enumerate(((gA, b_gA), (gB, b_gB), (gC, b_gC))):
                                S.op("pe", I_TR(ps[3][0:n, qi * MH:(qi + 1) * MH], src[:, c0:c0 + n], ident_f[0:MH, 0:MH]),
                                     reads=[sb, b_cst], writes=[psB[3]])
                            S.op("act", I_ACT(GRE[0:n, bi, :, :], ps[3][0:n, 0:3 * MH].rearrange("p (a b) -> p a b", a=3), AF.Copy),
                                 reads=[psB[3]], writes=[b_GRE])
                        S.op("dve", I_TT(Dg[:], mprev[:].unsqueeze(1).to_broadcast([MH, MH, NC + 1]),
                                         ident_f[0:MH, 0:MH].unsqueeze(2).to_broadcast([MH, MH, NC + 1]), ALU.mult),
                             reads=[b_mprev, b_cst], writes=[b_Dg])
                        ncol = MH * (NC + 1)
                        Dgf = Dg[:].rearrange("p a b -> p (a b)")
                        dbf = decb[:].rearrange("p a b -> p (a b)")
                        for x0 in range(0, ncol, 512):
                            n = min(512, ncol - x0)
                            S.op("pe", I_MM(ps[4][:, 0:n], ones_f[0:MH, :], Dgf[:, x0:x0 + n]), reads=[b_ones, b_Dg], writes=[psB[4]])
                            S.op("act", I_ACT(dbf[:, x0:x0 + n], ps[4][:, 0:n], AF.Copy), reads=[psB[4]], writes=[b_decb])
                        S.barrier()

                    qTm = T(s2, "qTm", [128, NA], BF16)
                    kTm = T(s2, "kTm", [128, NA], BF16)
                    tTm = T(s2, "tTm", [128, NA], BF16)
                    ktm = T(s2, "ktm", [128, NB + 1, 128], BF16)
                    vtm = T(s2, "vtm", [128, NB + 1, 129], BF16)
                    sotm = T(s2, "sotm", [128, NB + 1, 128], BF16)
                    hh = T(s2, "hh", [128, NB + 1, 128])
                    hsq = T(s2, "hsq", [128, NB + 1, 128], BF16)
                    nrm = T(s2, "nrm", [128, NB + 1, 128], BF16)
                    AT = T(s2, "AT", [128, 128], BF16)
                    kg = T(s2, "kg", [128, 128], BF16)
                    Cb = T(s2, "Cb", [128, 129], BF16)
                    Cf = T(s2, "Cf", [128, 129])
                    Cfs = T(s2, "Cfs", [128, 129])
                    sm = T(s2, "sm", [128, 8])
                    st1 = T(s2, "st1", [128, 4, NB + 1])
                    ycT = T(s2, "ycT", [128, NA], BF16)
                    (b_qTm, b_kTm, b_tTm, b_ktm, b_vtm, b_sotm, b_hh, b_hsq, b_nrm, b_AT, b_kg, b_Cb, b_Cf, b_Cfs, b_sm,
                     b_st1, b_ycT) = (Buf() for _ in range(17))
                    S.op("dve", I_MS(vtm[:, :, 128:129], 1.0), writes=[b_vtm])
                    S.op("dve", I_MS(hh[:], 0.0), writes=[b_hh])
                    for h in range(MH):
                        def cons_q(gi, c0, n, pap, pb):
                            S.op("act", I_ACT(qTm[:, c0:c0 + n], pap, AF.Copy), reads=[pb], writes=[b_qTm])
                        proj(w_in[l, OFF_M + h], 128, cons_q)

                        def cons_k(gi, c0, n, pap, pb):
                            S.op("act", I_AMUL(kTm[:, c0:c0 + n], pap, HS), reads=[pb], writes=[b_kTm])
                        proj(w_in[l, OFF_M + MH + h], 128, cons_k)

                        def tm_copy(dst, b_dst, src, b_src, ncols):
                            for bi, (c0, n) in enumerate(c.blocks):
                                pv = ps[3 + bi % 2][:].bitcast(BF16)
                                S.op("pe", I_TR(pv[0:n, 0:128], src[:, c0:c0 + n], ident_b), reads=[b_src, b_cst], writes=[psB[3 + bi % 2]])
                                S.op("act", I_ACT(dst[0:n, bi, 0:128], pv[0:n, 0:128], AF.Copy), reads=[psB[3 + bi % 2]], writes=[b_dst])
                        tm_copy(ktm, b_ktm, kTm, b_kTm, 128)

                        def cons_v(gi, c0, n, pap, pb):
                            S.op("act", I_ACT(tTm[:, c0:c0 + n], pap, AF.Copy), reads=[pb], writes=[b_tTm])
                        proj(w_in[l, OFF_M + 2 * MH + h], 128, cons_v)
                        tm_copy(vtm, b_vtm, tTm, b_tTm, 128)

                        def cons_o(gi, c0, n, pap, pb):
                            S.op("act", I_ACT(tTm[:, c0:c0 + n], pap, AF.Sigmoid), reads=[pb], writes=[b_tTm])
                        proj(w_in[l, OFF_M + 3 * MH + h], 128, cons_o)
                        tm_copy(sotm, b_sotm, tTm, b_tTm, 128)

                        S.op("dve", I_MS(Cf[:], 0.0), writes=[b_Cf])
                        S.dma("sp", "cst", I_DMA([(Cfs[:], cst_c[l, h])]), writes=[b_Cfs])
                        for bi, (c0, n) in enumerate(c.blocks):
                            S.op("pe", I_MM(ps[5][0:n, 0:n], kTm[:, c0:c0 + n], qTm[:, c0:c0 + n]), reads=[b_kTm, b_qTm], writes=[psB[5]])
                            S.op("dve", I_STT(AT[0:n, 0:n], ps[5][0:n, 0:n], GRE[0:n, bi, 0, h:h + 1], mask2_f[0:n, 0:n],
                                              ALU.mult, ALU.mult), reads=[psB[5], b_GRE, b_cst], writes=[b_AT])
                            if n == 128:
                                chunks = [(0, 64, 2 * bi), (64, 64, 2 * bi + 1)]
                                cf, bcf = Cf, b_Cf
                            else:
                                chunks = [(0, n, NC)]
                                cf, bcf = Cfs, b_Cfs
                            for (r0, rn, ci) in chunks:
                                dcol = decb[:, h, ci:ci + 1]
                                rows = slice(r0, r0 + rn)
                                S.op("act", I_ACT(Cb[:], cf[:], AF.Identity, scale=dcol), reads=[bcf, b_decb], writes=[b_Cb])
                                S.op("pe", I_MMS([
                                    (ps[6][rows, 0:129], qTm[:, c0 + r0:c0 + r0 + rn], Cb[:], True, False),
                                    (ps[6][rows, 0:129], AT[rows, r0:r0 + rn], vtm[rows, bi, :], False, True)]),
                                    reads=[b_qTm, b_Cb, b_AT, b_vtm], writes=[psB[6]])
                                S.op("dve", I_TS(kg[rows, :], ktm[rows, bi, :], GRE[rows, bi, 0, h:h + 1], None, ALU.mult),
                                     reads=[b_ktm, b_GRE], writes=[b_kg])
                                S.op("pe", I_MM(ps[7][:, 0:129], kg[rows, :], vtm[rows, bi, :]), reads=[b_kg, b_vtm], writes=[psB[7]])
                                S.op("dve", I_STT(cf[:], cf[:], dcol, ps[7][:, 0:129], ALU.mult, ALU.add),
                                     reads=[bcf, b_decb, psB[7]], writes=[bcf])
                                Rc = GRE[rows, bi, 1, h:h + 1]
                                Ec = GRE[rows, bi, 2, h:h + 1]
                                qn = ps[6][rows, 128:129]
                                S.op("act", I_ACT(sm[rows, 0:1], qn, AF.Abs), reads=[psB[6]], writes=[b_sm])
                                S.op("dve", I_STT(sm[rows, 1:2], sm[rows, 0:1], Rc, Ec, ALU.mult, ALU.max),
                                     reads=[b_sm, b_GRE], writes=[b_sm])
                                S.op("dve", I_RCP(sm[rows, 2:3], sm[rows, 1:2]), reads=[b_sm], writes=[b_sm])
                                S.op("dve", I_TT(sm[rows, 3:4], sm[rows, 2:3], Rc, ALU.mult), reads=[b_sm, b_GRE], writes=[b_sm])
                                S.op("dve", I_STT(hh[rows, bi, :], ps[6][rows, 0:128], sm[rows, 3:4], sotm[rows, bi, :],
                                                  ALU.mult, ALU.mult), reads=[psB[6], b_sm, b_sotm], writes=[b_hh])
                        S.dma("sp", "o_c", I_DMA([(c_o[l, 0, h], Cf[:]), (c_o[l, 1, h], Cfs[:])]), reads=[b_Cf, b_Cfs], n=2)
                        S.op("dve", I_RED(st1[:, 0, :], hh[:], ALU.add), reads=[b_hh], writes=[b_st1])
                        S.op("act", I_ACT(hsq[:], hh[:], AF.Square), reads=[b_hh], writes=[b_hsq])
                        S.op("dve", I_RED(st1[:, 1, :], hsq[:], ALU.add), reads=[b_hsq], writes=[b_st1])
                        S.op("dve", I_TS(st1[:, 0, :], st1[:, 0, :], 1.0 / 128, None, ALU.mult), reads=[b_st1], writes=[b_st1])
                        S.op("dve", I_TT(st1[:, 2, :], st1[:, 0, :], st1[:, 0, :], ALU.mult), reads=[b_st1], writes=[b_st1])
                        S.op("dve", I_STT(st1[:, 1, :], st1[:, 1, :], 1.0 / 128, st1[:, 2, :], ALU.mult, ALU.subtract),
                             reads=[b_st1], writes=[b_st1])
                        S.op("dve", I_TS(st1[:, 1, :], st1[:, 1, :], LN_EPS, None, ALU.add), reads=[b_st1], writes=[b_st1])
                        S.op("act", I_ACT(st1[:, 1, :], st1[:, 1, :], AF.Sqrt), reads=[b_st1], writes=[b_st1])
                        S.op("dve", I_RCP(st1[:, 1, :], st1[:, 1, :]), reads=[b_st1], writes=[b_st1])
                        for bi, (c0, n) in enumerate(c.blocks):
                            S.op("dve", I_TS(nrm[:, bi, :], hh[:, bi, :], st1[:, 0, bi:bi + 1], st1[:, 1, bi:bi + 1], ALU.subtract, ALU.mult),
                                 reads=[b_hh, b_st1], writes=[b_nrm])
                            pv = ps[3 + bi % 2][:].bitcast(BF16)
                            S.op("pe", I_TR(pv[:, 0:n], nrm[0:n, bi, :], ident_b[0:n, 0:n]), reads=[b_nrm, b_cst], writes=[psB[3 + bi % 2]])
                            S.op("act", I_ACT(ycT[:, c0:c0 + n], pv[:, 0:n], AF.Identity, scale=V[:, c.o_ng + h:c.o_ng + h + 1]),
                                 reads=[psB[3 + bi % 2], b_vec], writes=[b_ycT])
                        r0 = CC + AW + h * 128
                        S.dma("sp", "mixw", I_DMA([(mixT[r0:r0 + 128, :], ycT[:])]), reads=[b_ycT], writes=[b_mix[r0 // 128]])
                    S.barrier()
                S.enabled = True
                S.barrier()

        def phase2(l, x_src, x_src_id, x_dst, x_dst_id):
            V = vec[l]
            S.enabled = "p2" in c.stages
            TWM = max(w for _, w in c.tiles)
            NTC = (TWM + 127) // 128
            with ExitStack() as sc:
                xf = T(sc, "xf", [128, KD, TWM])
                ab = T(sc, "ab", [128, KD, TWM], BF16)
                NPW = 3
                pw = [T(sc, "pw%d" % i, [128, 4096], BF16) for i in range(NPW)]
                pwB = [Buf() for _ in range(NPW)]
                sall = T(sc, "sall", [128, NTC, 16, 128])
                skt = T(sc, "skt", [128, 16, 128], BF16)
                qb = T(sc, "qb", [128, 16, TWM], BF16)
                mr = T(sc, "mr", [128, 2, TWM])
                zb = T(sc, "zb", [128, TWM], BF16)
                zq = T(sc, "zq", [128, TWM], BF16)
                tt = T(sc, "tt", [128, 8, 128])
                ee = T(sc, "ee", [128, 1024], BF16)
                Gh = T(sc, "Gh", [128, 1024], BF16)
                Gacc = T(sc, "Gacc", [128, NTC, 1024], BF16)
                HT = T(sc, "HT", [128, 8, TWM], BF16)
                gl = T(sc, "gl", [128, TWM], BF16)
                T16 = T(sc, "T16", [128, 2, 16])
                tmp1 = T(sc, "tmp1", [128, 256])
                cand = T(sc, "cand", [128, 256])
                cv16 = T(sc, "cv16", [128, 8, 16])
                tau = T(sc, "tau", [128, NTC, 8])
                ccb = T(sc, "ccb", [128, NTC, 8])
                zz = T(sc, "zz", [128, 8])
                (b_xf, b_ab, b_sall, b_skt, b_qb, b_mr, b_zb, b_zq, b_tt, b_ee, b_Gh, b_Gacc, b_HT, b_gl, b_T16, b_tmp1,
                 b_cand, b_cv16, b_tau, b_ccb, b_zz) = (Buf() for _ in range(21))
                S.dma("pool", "subk", I_DMA([(skt[:], subk[l])], max_dma_last_dim=4096), writes=[b_skt])
                pst = SimpleNamespace(i=0)

                def load_pw(src, out_fn):
                    i = pst.i % NPW
                    pst.i += 1
                    S.dma("pool", "pw%d" % i, I_DMA([(out_fn(pw[i]), src)], max_dma_last_dim=4096), writes=[pwB[i]])
                    return pw[i], pwB[i]

                def as_w(t):
                    return t[:].rearrange("p (k m) -> p k m", m=128)[:, 0:KD, :]

                def layer_norm(TW, go, bo):
                    for co in range(KD):
                        S.op("act", I_ACT(zb[:, 0:TW], xf[:, co, 0:TW], AF.Copy), reads=[b_xf], writes=[b_zb])
                        S.op("act", I_ACT(zq[:, 0:TW], xf[:, co, 0:TW], AF.Square), reads=[b_xf], writes=[b_zq])
                        S.op("pe", I_MMS([(ps[0][:, 0:TW], ones_b[:], zb[:, 0:TW], co == 0, co == KD - 1)]),
                             reads=[b_ones, b_zb], writes=[psB[0]])
                        S.op("pe", I_MMS([(ps[1][:, 0:TW], ones_b[:], zq[:, 0:TW], co == 0, co == KD - 1)]),
                             reads=[b_ones, b_zq], writes=[psB[1]])
                    m_, r_ = mr[:, 0, 0:TW], mr[:, 1, 0:TW]
                    S.op("act", I_AMUL(m_, ps[0][:, 0:TW], 1.0 / D), reads=[psB[0]], writes=[b_mr])
                    S.op("dve", I_TT(r_, m_, m_, ALU.mult), reads=[b_mr], writes=[b_mr])
                    S.op("dve", I_STT(r_, ps[1][:, 0:TW], 1.0 / D, r_, ALU.mult, ALU.subtract), reads=[psB[1], b_mr], writes=[b_mr])
                    S.op("dve", I_TS(r_, r_, LN_EPS, None, ALU.add), reads=[b_mr], writes=[b_mr])
                    S.op("act", I_ACT(r_, r_, AF.Sqrt), reads=[b_mr], writes=[b_mr])
                    S.op("dve", I_RCP(r_, r_), reads=[b_mr], writes=[b_mr])
                    for co in range(KD):
                        z = xf[:, co, 0:TW]
                        S.op("dve", I_TT(z, z, m_, ALU.subtract), reads=[b_xf, b_mr], writes=[b_xf])
                        S.op("dve", I_TT(z, z, r_, ALU.mult), reads=[b_xf, b_mr], writes=[b_xf])
                        S.op("dve", I_TS(z, z, V[:, go + co:go + co + 1], V[:, bo + co:bo + co + 1], ALU.mult, ALU.add),
                             reads=[b_xf, b_vec], writes=[b_xf])
                        S.op("act", I_ACT(ab[:, co, 0:TW], z, AF.Copy), reads=[b_xf], writes=[b_ab])

                for ti, (t0, TW) in enumerate(c.tiles):
                    tcs = [(a, min(128, TW - a)) for a in range(0, TW, 128)]
                    xsv = x_src.rearrange("(k p) n -> p k n", p=128)
                    kq = [(k0, min(KD, k0 + 8)) for k0 in range(0, KD, 8)]
                    S.dma("sp", "xf", I_DMA([(xf[:, a:b_, 0:TW], xsv[:, a:b_, t0:t0 + TW]) for (a, b_) in kq]),
                          reads=[xbufs(x_src_id)[ti]], writes=[b_xf], n=len(kq))
                    mxv = mixT.rearrange("(k p) n -> p k n", p=128)
                    S.dma("sp", "mixb", I_DMA([(ab[:, a:b_, 0:TW], mxv[:, a:b_, t0:t0 + TW]) for (a, b_) in kq]),
                          reads=b_mix, writes=[b_ab], n=len(kq))
                    for co in range(KD):
                        wt, wtb = load_pw(w_out[l, co], as_w)
                        wv = as_w(wt)
                        pi = 2 + co % 2
                        S.op("pe", I_MMS([(ps[pi][:, 0:TW], wv[:, k, :], ab[:, k, 0:TW], k == 0, k == KD - 1) for k in range(KD)]),
                             reads=[wtb, b_ab], writes=[psB[pi]])
                        S.op("dve", I_STT(xf[:, co, 0:TW], xf[:, co, 0:TW], c.alpha, ps[pi][:, 0:TW], ALU.mult, ALU.add),
                             reads=[b_xf, psB[pi]], writes=[b_xf])
                    layer_norm(TW, c.o_ln1g, c.o_ln1b)
                    for c16 in range(16):
                        wt, wtb = load_pw(wq[l, c16], as_w)
                        wv = as_w(wt)
                        pi = 2 + c16 % 2
                        S.op("pe", I_MMS([(ps[pi][:, 0:TW], wv[:, k, :], ab[:, k, 0:TW], k == 0, k == KD - 1) for k in range(KD)]),
                             reads=[wtb, b_ab], writes=[psB[pi]])
                        S.op("act", I_ACT(qb[:, c16, 0:TW], ps[pi][:, 0:TW], AF.Copy), reads=[psB[pi]], writes=[b_qb])
                    for tci, (a0, nt) in enumerate(tcs):
                        for qd in range(4):
                            pi = 2 + qd % 2
                            S.op("pe", I_MMS([(ps[pi][0:nt, j * 128:(j + 1) * 128], qb[:, qd * 4 + j, a0:a0 + nt], skt[:, qd * 4 + j, :],
                                               True, True) for j in range(4)]), reads=[b_qb, b_skt], writes=[psB[pi]])
                            S.op("act", I_ACT(sall[0:nt, tci, qd * 4:qd * 4 + 4, :],
                                              ps[pi][0:nt, :].rearrange("p (a b) -> p a b", a=4), AF.Copy),
                                 reads=[psB[pi]], writes=[b_sall])
                        for h in range(8):
                            for p_ in range(2):
                                src = sall[0:nt, tci, 2 * h + p_, :]
                                S.op("dve", I_MAX(T16[0:nt, p_, 0:8], src), reads=[b_sall], writes=[b_T16])
                                S.op("dve", I_MR(tmp1[0:nt, 0:128], T16[0:nt, p_, 0:8], src, NEG), reads=[b_sall, b_T16], writes=[b_tmp1])
                                S.op("dve", I_MAX(T16[0:nt, p_, 8:16], tmp1[0:nt, 0:128]), reads=[b_tmp1], writes=[b_T16])
                            cd3 = cand[0:nt, :].rearrange("p (a b) -> p a b", a=16)
                            S.op("dve", I_TT(cd3, T16[0:nt, 0, :].unsqueeze(2).to_broadcast([nt, 16, 16]),
                                             T16[0:nt, 1, :].unsqueeze(1).to_broadcast([nt, 16, 16]), ALU.add),
                                 reads=[b_T16], writes=[b_cand])
                            S.op("dve", I_MAX(cv16[0:nt, h, 0:8], cand[0:nt, :]), reads=[b_cand], writes=[b_cv16])
                            S.op("dve", I_MR(tmp1[0:nt, :], cv16[0:nt, h, 0:8], cand[0:nt, :], NEG), reads=[b_cand, b_cv16], writes=[b_tmp1])
                            S.op("dve", I_MAX(cv16[0:nt, h, 8:16], tmp1[0:nt, :]), reads=[b_tmp1], writes=[b_cv16])
                        S.op("dve", I_CP(tau[0:nt, tci, :], cv16[0:nt, :, 15]), reads=[b_cv16], writes=[b_tau])
                        S.op("dve", I_CP(ccb[0:nt, tci, :], cv16[0:nt, :, 0]), reads=[b_cv16], writes=[b_ccb])
                        S.op("dve", I_TT(cv16[0:nt, :, :], cv16[0:nt, :, :], ccb[0:nt, tci, :].unsqueeze(2).to_broadcast([nt, 8, 16]),
                                         ALU.subtract), reads=[b_cv16, b_ccb], writes=[b_cv16])
                        S.op("act", I_ACT(cv16[0:nt, :, :], cv16[0:nt, :, :], AF.Exp), reads=[b_cv16], writes=[b_cv16])
                        S.op("dve", I_RED(zz[0:nt, :], cv16[0:nt, :, :], ALU.add), reads=[b_cv16], writes=[b_zz])
                        S.op("act", I_ACT(zz[0:nt, :], zz[0:nt, :], AF.Ln), reads=[b_zz], writes=[b_zz])
                        S.op("dve", I_STT(ccb[0:nt, tci, :], ccb[0:nt, tci, :], -1.0, zz[0:nt, :], ALU.mult, ALU.subtract),
                             reads=[b_ccb, b_zz], writes=[b_ccb])
                    xfl = xf[:, :, 0:TW]
                    S.op("dve", I_TS(xfl, xfl, c.alpha, None, ALU.mult), reads=[b_xf], writes=[b_xf])
                    for g in range(16):
                        for tci, (a0, nt) in enumerate(tcs):
                            for h in range(8):
                                t3 = tt[0:nt, :, :]
                                S.op("dve", I_TT(t3, sall[0:nt, tci, 2 * h, g * 8:(g + 1) * 8].unsqueeze(2).to_broadcast([nt, 8, 128]),
                                                 sall[0:nt, tci, 2 * h + 1, :].unsqueeze(1).to_broadcast([nt, 8, 128]), ALU.add),
                                     reads=[b_sall], writes=[b_tt])
                                t2 = tt[0:nt, :, :].rearrange("p a b -> p (a b)")
                                S.op("act", I_ACT(ee[0:nt, :], t2, AF.Exp, bias=ccb[0:nt, tci, h:h + 1]), reads=[b_tt, b_ccb], writes=[b_ee])
                                dst, bd = (Gacc[0:nt, tci, :], b_Gacc) if h == 0 else (Gh[0:nt, :], b_Gh)
                                S.op("dve", I_STT(dst, t2, tau[0:nt, tci, h:h + 1], ee[0:nt, :], ALU.is_ge, ALU.mult),
                                     reads=[b_tt, b_tau, b_ee], writes=[bd])
                                if h > 0:
                                    S.op("dve", I_TT(Gacc[0:nt, tci, :], Gacc[0:nt, tci, :], Gh[0:nt, :], ALU.add),
                                         reads=[b_Gacc, b_Gh], writes=[b_Gacc])
                        for ic in range(8):
                            ec = g * 8 + ic
                            wt, wtb = load_pw(uT[l, ec], as_w)
                            wv = as_w(wt)
                            pi = 2 + ic % 2
                            S.op("pe", I_MMS([(ps[pi][:, 0:TW], wv[:, k, :], ab[:, k, 0:TW], k == 0, k == KD - 1) for k in range(KD)]),
                                 reads=[wtb, b_ab], writes=[psB[pi]])
                            S.op("act", I_ACT(gl[:, 0:TW], ps[pi][:, 0:TW], AF.Gelu), reads=[psB[pi]], writes=[b_gl])
                            pg = ps[0][:].bitcast(BF16)
                            for tci, (a0, nt) in enumerate(tcs):
                                S.op("pe", I_TR(pg[:, a0:a0 + nt], Gacc[0:nt, tci, ic * 128:(ic + 1) * 128], ident_b[0:nt, 0:nt]),
                                     reads=[b_Gacc, b_cst], writes=[psB[0]])
                            S.op("dve", I_TT(HT[:, ic, 0:TW], gl[:, 0:TW], pg[:, 0:TW], ALU.mult), reads=[b_gl, psB[0]], writes=[b_HT])
                        NDS = (D + 511) // 512
                        for ds in range(NDS):
                            dw = min(512, D - ds * 512)
                            src = vtab[l, g * 8:(g + 1) * 8, :, ds * 512:ds * 512 + dw].rearrange("c p d -> p c d")
                            wt, wtb = load_pw(src, lambda t, dw=dw: t[:].rearrange("p (c d) -> p c d", c=8)[:, :, 0:dw])
                            wv = wt[:].rearrange("p (c d) -> p c d", c=8)
                            ndc = dw // 128
                            items = []
                            for ic in range(8):
                                for dc in range(ndc):
                                    items.append((ps[4 + dc][:, 0:TW], wv[:, ic, dc * 128:(dc + 1) * 128], HT[:, ic, 0:TW], ic == 0, ic == 7))
                            S.op("pe", I_MMS(items), reads=[wtb, b_HT], writes=[psB[4 + dc] for dc in range(ndc)])
                            for dc in range(ndc):
                                co = ds * 4 + dc
                                S.op("dve", I_TT(xf[:, co, 0:TW], xf[:, co, 0:TW], ps[4 + dc][:, 0:TW], ALU.add),
                                     reads=[b_xf, psB[4 + dc]], writes=[b_xf])
                    layer_norm(TW, c.o_ln2g, c.o_ln2b)
                    xdv = x_dst.rearrange("(k p) n -> p k n", p=128)
                    S.dma("sp", "xo", I_DMA([(xdv[:, a:b_, t0:t0 + TW], xf[:, a:b_, 0:TW]) for (a, b_) in kq]),
                          reads=[b_xf], writes=[xbufs(x_dst_id)[ti]], n=len(kq))
                S.enabled = True
                S.barrier()

        srcs = [(xT_in, "in")]
        for l in range(c.nlayers_run):
            x_src, sid = srcs[-1]
            if l == L - 1:
                x_dst, did = yT, "y"
            else:
                x_dst, did = xsc[l % 2], "sc%d" % (l % 2)
            phase1(l, x_src, sid)
            if "p2" in c.stages:
                phase2(l, x_src, sid, x_dst, did)
            srcs.append((x_dst, did))
        S.emit()
    return nc


def _cmask_table(c):
    p = np.arange(128)[:, None]
    x = np.arange(c.XW)[None, :]
    d = x - p - 384
    m = ((d >= 0) & (d <= 128)).astype(np.float32)
    m += ((d >= 0) & (d % 4 == 0) & (d <= 512)).astype(np.float32)
    m += ((d >= 0) & (d % 16 == 0) & (d <= 2048)).astype(np.float32)
    return np.ascontiguousarray(m, dtype=np.float32)


def _consts():
    ident = np.eye(128, dtype=np.float32)
    s = np.arange(128)[:, None]
    t = np.arange(128)[None, :]
    mask2 = ((s // 64 == t // 64) & (s <= t)).astype(np.float32)
    return np.ascontiguousarray(np.concatenate([ident, mask2], axis=1))


def prep_shared(c, inp):
    L, KD, D = c.L, c.KD, c.D
    f = lambda a: np.ascontiguousarray(a, dtype=np.float32)
    sh = {}
    w_in = np.asarray(inp["w_in"])
    ncol = c.NCHI * 128
    sh["w_in"] = f(w_in[:, :, :ncol].reshape(L, KD, 128, c.NCHI, 128).transpose(0, 3, 2, 1, 4))
    sh["w_gate"] = f(w_in[:, :, ncol:].reshape(L, KD, 128, 2, c.MH).transpose(0, 3, 2, 1, 4))
    sh["w_out"] = f(np.asarray(inp["w_out"]).reshape(L, KD, 128, KD, 128).transpose(0, 3, 2, 1, 4))
    sh["wq"] = f(np.asarray(inp["peer_wq"]).reshape(L, KD, 128, 16, 128).transpose(0, 3, 2, 1, 4))
    pu = np.asarray(inp["peer_u"]).reshape(L, 128, 128, KD, 128)
    sh["uT"] = f(pu.transpose(0, 1, 4, 3, 2))
    sh["vtab"] = f(np.asarray(inp["peer_v"]).reshape(L, 128, 128, D))
    sk = np.asarray(inp["peer_subkeys"]).reshape(L, 16, 128, 128)
    sh["subk"] = f(sk.transpose(0, 3, 1, 2))
    cols = []

    def fm(a, n):
        return np.asarray(a).reshape(L, n, 128).transpose(0, 2, 1)
    cols.append(fm(inp["ln1_g"], KD))
    cols.append(fm(inp["ln1_b"], KD))
    cols.append(fm(inp["ln2_g"], KD))
    cols.append(fm(inp["ln2_b"], KD))
    cols.append(fm(inp["conv_b"], c.NCC))
    cols.append(fm(inp["conv_ln_g"], c.NCC))
    cols.append(fm(inp["conv_ln_b"], c.NCC))
    cw = np.asarray(inp["conv_w"]).reshape(L, 31, c.NCC, 128).transpose(0, 3, 2, 1).reshape(L, 128, c.NCC * 31)
    cols.append(cw)
    cols.append(np.asarray(inp["mlstm_norm_g"]).transpose(0, 2, 1))
    sh["vecs"] = f(np.concatenate(cols, axis=2))
    sh["gbias"] = f(np.stack([np.asarray(inp["mlstm_b_i"]), np.asarray(inp["mlstm_b_f"])], axis=2))
    sh["consts"] = _consts()
    sh["cmask"] = _cmask_table(c)
    return sh


def prep_core(c, inp, core):
    L = c.L
    f = lambda a: np.ascontiguousarray(a, dtype=np.float32)
    b = core % c.B
    d = {}
    xp = np.asarray(inp["x_prompt"])[b]
    xs = np.asarray(inp["x_sample"])[core]
    d["xT"] = f(np.concatenate([xp, xs], axis=0).T)
    ck = np.asarray(inp["cache_attn_k"])[:, core]
    d["ckT"] = f(ck.transpose(0, 2, 3, 1))
    d["cvv"] = f(np.asarray(inp["cache_attn_v"])[:, core].reshape(L, c.W, c.AW))
    d["cconvT"] = f(np.asarray(inp["state_conv"])[:, core].transpose(0, 2, 1))
    cc_ = np.asarray(inp["state_mlstm_c"])[:, core]
    nn_ = np.asarray(inp["state_mlstm_n"])[:, core][..., None]
    d["cst_c"] = f(np.concatenate([cc_, nn_], axis=3))
    d["cst_m"] = f(np.asarray(inp["state_mlstm_m"])[:, core][..., None])
    return d


def assemble(c, res):
    L, S_, TS, AH, MH = c.L, c.S, c.TS, c.AH, c.MH
    B, DB = c.B, c.DB
    R = [r for r in res]
    y_p = np.stack([R[b]["yT"][:, :S_].T for b in range(B)])
    y_s = np.stack([R[i]["yT"][:, S_:].T for i in range(DB)])

    def kv(name, cores, lo, hi):
        out = []
        for i in cores:
            a = R[i][name][:, :, lo:hi]
            out.append(a.transpose(0, 2, 1).reshape(L, hi - lo, AH, 128))
        return np.stack(out, axis=1)
    k_p = kv("kT_o", range(B), 0, S_)
    v_p = kv("vT_o", range(B), 0, S_)
    k_s = kv("kT_o", range(DB), S_, S_ + TS)
    v_s = kv("vT_o", range(DB), S_, S_ + TS)
    cv_p = np.stack([R[b]["convT_o"][:, 0].transpose(0, 2, 1) for b in range(B)], axis=1)
    cv_s = np.stack([R[i]["convT_o"][:, 1].transpose(0, 2, 1) for i in range(DB)], axis=1)
    c_p = np.stack([R[b]["c_o"][:, 0, :, :, :128] for b in range(B)], axis=1)
    c_s = np.stack([R[i]["c_o"][:, 1, :, :, :128] for i in range(DB)], axis=1)
    n_p = np.stack([R[b]["c_o"][:, 0, :, :, 128] for b in range(B)], axis=1)
    n_s = np.stack([R[i]["c_o"][:, 1, :, :, 128] for i in range(DB)], axis=1)
    m_p = np.stack([R[b]["m_o"][:, :, 0] for b in range(B)], axis=1)
    m_s = np.stack([R[i]["m_o"][:, :, 1] for i in range(DB)], axis=1)
    outs = (y_p, y_s, k_p, v_p, k_s, v_s, cv_p, cv_s, c_p, c_s, n_p, n_s, m_p, m_s)
    return tuple(np.ascontiguousarray(o, dtype=np.float32) for o in outs)


def run(c, inputs):
    nc = build(c)
    sh = prep_shared(c, inputs)
    in_maps = []
    for core in range(8):
        d = dict(sh)
        d.update(prep_core(c, inputs, core))
        in_maps.append(d)
    res = run_bass_kernel_spmd(nc, in_maps, core_ids=list(range(8)))
    return assemble(c, res.results)


def kernel(**inputs):
    c = mkcfg()
    return run(c, inputs)
```
